# Optimizing a Trainium2 kernel written in Bass

```python
import jax, jax.numpy as jnp
from jax import lax
import numpy as np

D_MODEL = 2048
BATCH = 4
SEQ = 2048
DEPTH = 1

HEAD_DIM = 64
N_Q_HEADS = (D_MODEL // 2) // HEAD_DIM
N_KV_HEADS = N_Q_HEADS // 4
GROUP = N_Q_HEADS // N_KV_HEADS
ATTN_WIDTH = N_Q_HEADS * HEAD_DIM
KV_WIDTH = N_KV_HEADS * HEAD_DIM
CONV_WIDTH = D_MODEL // 2
CONV_K = 3
WINDOW = 128
BLOCK = 128
ROPE_THETA = 500000.0
ROT_DIM = HEAD_DIM // 4
D_FF = ((8 * D_MODEL // 3 + 255) // 256) * 256
RMS_EPS = 1e-6
ATTN_SCALE = HEAD_DIM ** -0.5
NEG_INF = -1e30
IN_WIDTHS = (CONV_WIDTH, CONV_WIDTH, CONV_WIDTH, ATTN_WIDTH, KV_WIDTH, KV_WIDTH, D_MODEL, D_MODEL)
IN_SPLITS = tuple(int(s) for s in np.cumsum(IN_WIDTHS)[:-1])
IN_TOTAL = int(sum(IN_WIDTHS))

kernel_name = "hybrid_macaron_conv_swa_gated"


def rms_norm(x, g):
    xf = x.astype(jnp.float32)
    y = xf * lax.rsqrt(jnp.mean(xf * xf, axis=-1, keepdims=True) + RMS_EPS)
    return (y * g.astype(jnp.float32)).astype(x.dtype)


def swiglu(h, w_gu, w_down):
    g, u = jnp.split(h @ w_gu, 2, axis=-1)
    return (jax.nn.silu(g) * u) @ w_down


def rope_tables(seq_len):
    inv_freq = 1.0 / (ROPE_THETA ** (jnp.arange(0, ROT_DIM, 2, dtype=jnp.float32) / ROT_DIM))
    ang = jnp.arange(seq_len, dtype=jnp.float32)[:, None] * inv_freq[None, :]
    return jnp.cos(ang)[None, :, None, :], jnp.sin(ang)[None, :, None, :]


def partial_rope(x, cos, sin):
    half = ROT_DIM // 2
    xf = x.astype(jnp.float32)
    x1, x2, xp = xf[..., :half], xf[..., half:ROT_DIM], xf[..., ROT_DIM:]
    out = jnp.concatenate([x1 * cos - x2 * sin, x2 * cos + x1 * sin, xp], axis=-1)
    return out.astype(x.dtype)


def causal_short_conv(u, w):
    S = u.shape[1]
    up = jnp.pad(u, ((0, 0), (CONV_K - 1, 0), (0, 0)))
    y = up[:, 0:S] * w[0]
    for j in range(1, CONV_K):
        y = y + up[:, j:j + S] * w[j]
    return y


def banded(t, nb):
    B = t.shape[0]
    tp = jnp.pad(t, ((0, 0), (BLOCK, 0), (0, 0), (0, 0)))
    tb = tp.reshape(B, nb + 1, BLOCK, t.shape[2], t.shape[3])
    return jnp.concatenate([tb[:, :-1], tb[:, 1:]], axis=2)


def sliding_window_gqa_sinks(q, k, v, sinks):
    B, S = q.shape[0], q.shape[1]
    nb = S // BLOCK
    qb = q.reshape(B, nb, BLOCK, N_KV_HEADS, GROUP, HEAD_DIM)
    kb, vb = banded(k, nb), banded(v, nb)
    s = jnp.einsum('bnqhgd,bnkhd->bnhgqk', qb, kb).astype(jnp.float32) * ATTN_SCALE
    qi = jnp.arange(BLOCK)[:, None] + BLOCK
    ki = jnp.arange(2 * BLOCK)[None, :]
    diff = qi - ki
    in_window = (diff >= 0) & (diff < WINDOW)
    key_pos = jnp.arange(nb)[:, None] * BLOCK + jnp.arange(2 * BLOCK)[None, :] - BLOCK
    valid = in_window[None] & (key_pos >= 0)[:, None, :]
    s = jnp.where(valid[None, :, None, None], s, NEG_INF)
    sink = sinks.astype(jnp.float32).reshape(1, 1, N_KV_HEADS, GROUP, 1, 1)
    m = jnp.maximum(jnp.max(s, axis=-1, keepdims=True), sink)
    p = jnp.exp(s - m)
    denom = jnp.sum(p, axis=-1, keepdims=True) + jnp.exp(sink - m)
    o = jnp.einsum('bnhgqk,bnkhd->bnqhgd', p / denom, vb.astype(jnp.float32))
    return o.reshape(B, S, ATTN_WIDTH).astype(q.dtype)


def hybrid_mixer(h, w_in, conv_w, q_norm_g, k_norm_g, sinks, w_out_conv, w_out_attn, w_o, cos, sin):
    B, S, _ = h.shape
    xc, bg, cg, q, k, v, ga, gb = jnp.split(h @ w_in, IN_SPLITS, axis=-1)
    ya = (bg * causal_short_conv(cg * xc, conv_w)) @ w_out_conv
    q = rms_norm(q.reshape(B, S, N_Q_HEADS, HEAD_DIM), q_norm_g)
    k = rms_norm(k.reshape(B, S, N_KV_HEADS, HEAD_DIM), k_norm_g)
    v = v.reshape(B, S, N_KV_HEADS, HEAD_DIM)
    q = partial_rope(q, cos, sin)
    k = partial_rope(k, cos, sin)
    yb = sliding_window_gqa_sinks(q, k, v, sinks) @ w_out_attn
    merged = jax.nn.sigmoid(ga) * ya + jax.nn.sigmoid(gb) * yb
    return merged @ w_o


def setup_inputs(seed: int = 0) -> dict:
    key = jax.random.key(seed)
    ks = jax.random.split(key, 18)
    f32 = jnp.float32

    def w(k, shape, fan_in):
        return jax.random.normal(k, shape, f32) * (fan_in ** -0.5)

    def gain(k, n):
        return 1.0 + 0.02 * jax.random.normal(k, (DEPTH, n), f32)

    return {
        "x": jax.random.normal(ks[0], (BATCH, SEQ, D_MODEL), f32),
        "g_ffn1": gain(ks[1], D_MODEL),
        "w_gu1": w(ks[2], (DEPTH, D_MODEL, 2 * D_FF), D_MODEL),
        "w_down1": w(ks[3], (DEPTH, D_FF, D_MODEL), D_FF),
        "g_mix": gain(ks[4], D_MODEL),
        "w_in": w(ks[5], (DEPTH, D_MODEL, IN_TOTAL), D_MODEL),
        "conv_w": w(ks[6], (DEPTH, CONV_K, CONV_WIDTH), CONV_K),
        "q_norm_g": gain(ks[7], HEAD_DIM),
        "k_norm_g": gain(ks[8], HEAD_DIM),
        "sinks": 0.5 * jax.random.normal(ks[9], (DEPTH, N_Q_HEADS), f32),
        "w_out_conv": w(ks[10], (DEPTH, CONV_WIDTH, D_MODEL), CONV_WIDTH),
        "w_out_attn": w(ks[11], (DEPTH, ATTN_WIDTH, D_MODEL), ATTN_WIDTH),
        "w_o": w(ks[12], (DEPTH, D_MODEL, D_MODEL), D_MODEL),
        "g_ffn2": gain(ks[13], D_MODEL),
        "w_gu2": w(ks[14], (DEPTH, D_MODEL, 2 * D_FF), D_MODEL),
        "w_down2": w(ks[15], (DEPTH, D_FF, D_MODEL), D_FF),
    }


def reference(x, g_ffn1, w_gu1, w_down1, g_mix, w_in, conv_w, q_norm_g, k_norm_g, sinks,
              w_out_conv, w_out_attn, w_o, g_ffn2, w_gu2, w_down2):
    cos, sin = rope_tables(x.shape[1])
    for l in range(DEPTH):
        x = x + 0.5 * swiglu(rms_norm(x, g_ffn1[l]), w_gu1[l], w_down1[l])
        x = x + hybrid_mixer(rms_norm(x, g_mix[l]), w_in[l], conv_w[l], q_norm_g[l], k_norm_g[l],
                             sinks[l], w_out_conv[l], w_out_attn[l], w_o[l], cos, sin)
        x = x + 0.5 * swiglu(rms_norm(x, g_ffn2[l]), w_gu2[l], w_down2[l])
    return x
```

```python
import numpy as np
from contextlib import ExitStack
import concourse.bass as bass
import concourse.mybir as mybir
from concourse.bass_utils import run_bass_kernel_spmd

F32 = mybir.dt.float32
BF16 = mybir.dt.bfloat16
AF = mybir.ActivationFunctionType
ALU = mybir.AluOpType
AX = mybir.AxisListType

D = 2048
SEQ = 2048
BATCH = 4
DFF = 5632
NCH = D // 128
TOWN = 1024
HALO = 128
TT = TOWN + HALO
BLK = [(0, 128), (128, 640), (640, 1152)]
NQ = 4
FQ = 11
DG = 2
DGW = DG * 128
NDG = NCH // DG
RMS_EPS = 1e-6
NEG = -30000.0
STOP_AFTER = 3
START_AT = 1
NCORES = 8


def _tile_blk(t):
    return 0 if t == 0 else (1 if t <= 4 else 2)


class Prog:
    def __init__(self, nc, es):
        self.nc = nc
        self.es = es
        self.planning = True
        self.q = {k: [] for k in ("pe", "act", "dve", "pool", "sp")}
        self.sem = {k: es.enter_context(nc.semaphore(f"prog_{k}")) for k in ("pe", "act", "dve")}
        self.cnt = {k: 0 for k in self.sem}
        self.dma_cnt = {}
        self.lw = {}
        self.rd = {}
        self.fence_deps = {}
        self.epoch = 0
        self.W = []
        self.wi = 0
        self.wnext = 0
        self.pools = {}
        self.bank = 0
        self.flips = {}
        self.nsem = 0
        self.hold = 0

    def new_sem(self, name):
        self.nsem += 1
        return self.es.enter_context(self.nc.semaphore(f"{name}_{self.nsem}"))

    def add_pool(self, name, tiles):
        self.pools[name] = dict(tiles=tiles, nb=len(tiles), k=0, hist=[],
                                sems=[self.new_sem(f"w{name}") for _ in tiles])

    def start_emit(self):
        self.planning = False
        self.wi = 0
        self.wnext = 0
        self.epoch = 0
        self.bank = 0
        self.flips = {}
        for p in self.pools.values():
            p["k"] = 0
            p["hist"] = []

    def nb(self):
        b = self.bank
        self.bank = (self.bank + 1) % 8
        return b

    def flip(self, name, n=2):
        v = self.flips.get(name, 0)
        self.flips[name] = (v + 1) % n
        return v

    def _deps(self, reads, writes):
        deps = dict(self.fence_deps)

        def add(s, v):
            if deps.get(s, 0) < v:
                deps[s] = v
        for k in reads:
            ev = self.lw.get(k)
            if ev is not None:
                add(*ev)
        for k in writes:
            ev = self.lw.get(k)
            if ev is not None:
                add(*ev)
            for s, v in self.rd.get(k, {}).items():
                add(s, v)
        return deps

    def _record(self, ev, reads, writes):
        for k in reads:
            d = self.rd.setdefault(k, {})
            if d.get(ev[0], 0) < ev[1]:
                d[ev[0]] = ev[1]
        for k in writes:
            self.lw[k] = ev
            self.rd[k] = {}

    def op(self, eng, fn, reads=(), writes=()):
        if self.planning:
            return
        psr = [k for k in reads if k[0] == "ps"]
        if psr:
            reads = [k for k in reads if k[0] != "ps"]
            writes = list(writes) + psr
        deps = self._deps(reads, writes)
        self.cnt[eng] += 1
        ev = (self.sem[eng], self.cnt[eng])
        self.q[eng].append((deps, fn, ev, 1))
        self._record(ev, reads, writes)

    def dma(self, eng, out, in_, sem, reads=(), writes=()):
        if self.planning:
            return
        deps = self._deps(reads, writes)
        v = self.dma_cnt.get(sem, 0) + 16
        self.dma_cnt[sem] = v
        ev = (sem, v)
        self.q[eng].append((deps, (lambda E: E.dma_start(out=out, in_=in_)), ev, 16))
        self._record(ev, reads, writes)
        return ev

    def wait_only(self, eng, ev):
        if self.planning:
            return
        self.q[eng].append(({ev[0]: ev[1]}, None, None, 0))

    def fence(self):
        self.epoch += 1
        if self.planning:
            return
        fd = {self.sem[e]: self.cnt[e] for e in self.sem if self.cnt[e] > 0}
        for s, v in self.dma_cnt.items():
            fd[s] = v
        self.fence_deps = fd

    def use(self, pool, src):
        P = self.pools[pool]
        if self.planning:
            self.W.append(dict(pool=pool, src=src, epoch=self.epoch))
            k = P["k"]
            P["k"] += 1
            return P["tiles"][k % P["nb"]], (pool, k % P["nb"])
        i = self.wi
        self.wi += 1
        assert self.W[i]["pool"] == pool
        self._issue_loads(i)
        slot = self.W[i]["slot"]
        return P["tiles"][slot], (pool, slot)

    def _issue_loads(self, i):
        while self.wnext < len(self.W):
            s = self.W[self.wnext]
            if s["epoch"] > self.epoch:
                break
            P = self.pools[s["pool"]]
            k = len(P["hist"])
            if k >= P["nb"] and P["hist"][k - P["nb"]] >= i - self.hold:
                break
            slot = k % P["nb"]
            s["slot"] = slot
            P["hist"].append(self.wnext)
            t = P["tiles"][slot]
            nfree = s["src"].shape[-1]
            dst = t[:].rearrange("p a b -> p (a b)")[:, 0:nfree]
            self.dma("pool", dst, s["src"], P["sems"][slot], writes=[(s["pool"], slot)])
            self.wnext += 1

    def act(self, out, in_, func, reads, writes, bias=None, scale=None):
        kw = {}
        if bias is not None:
            kw["bias"] = bias
        if scale is not None:
            kw["scale"] = scale
        self.op("act", lambda E: E.activation(out=out, in_=in_, func=func, **kw), reads, writes)

    def tt(self, out, in0, in1, op, reads, writes):
        self.op("dve", lambda E: E.tensor_tensor(out=out, in0=in0, in1=in1, op=op), reads, writes)

    def ts(self, out, in0, s1, op0, reads, writes, s2=None, op1=None):
        if op1 is None:
            self.op("dve", lambda E: E.tensor_scalar(out=out, in0=in0, scalar1=s1, scalar2=None, op0=op0),
                    reads, writes)
        else:
            self.op("dve", lambda E: E.tensor_scalar(out=out, in0=in0, scalar1=s1, scalar2=s2, op0=op0, op1=op1),
                    reads, writes)

    def stt(self, out, in0, scalar, in1, op0, op1, reads, writes):
        self.op("dve", lambda E: E.scalar_tensor_tensor(out=out, in0=in0, scalar=scalar, in1=in1,
                                                        op0=op0, op1=op1), reads, writes)

    def mm(self, mms, reads, writes):
        def fn(E):
            ins = None
            for (o, l, r, st, sp) in mms:
                ins = E.matmul(o, l, r, start=st, stop=sp)
            return ins
        self.op("pe", fn, reads, writes)

    def transposes(self, trs, ident, reads, writes):
        def fn(E):
            ins = None
            for (o, i) in trs:
                ins = E.transpose(o, i, ident)
            return ins
        self.op("pe", fn, reads, writes)

    def emit_engine(self, name, E):
        waited = {}
        mysem = self.sem.get(name)
        for deps, fn, ev, inc in self.q[name]:
            for s, v in deps.items():
                if name == "pe" and mysem is not None and s == mysem:
                    continue
                if waited.get(s, 0) >= v:
                    continue
                E.wait_ge(s, v)
                waited[s] = v
            if fn is None:
                continue
            ins = fn(E)
            if ev is not None:
                ins.then_inc(ev[0], inc)


def build_nc(stop_after=3, start_at=1):
    nc = bass.Bass("TRN2", target_bir_lowering=False)
    es = ExitStack()
    P = Prog(nc, es)

    def din(name, shape):
        return nc.dram_tensor(name, shape, F32, kind="ExternalInput").ap()

    xT_d = din("xT", [128, NCH * TT]).rearrange("p (c t) -> p c t", c=NCH)
    gains_d = din("gains", [128, 48])
    qkg_d = din("qkg", [128, 2])
    convw_d = din("convw", [128, 24])
    sinks_d = din("sinks", [128, 16])
    cs_d = din("cs", [128, 2 * TT])
    mask_d = din("mask", [128, 512])
    cmat_d = din("cmat", [128, 512])
    need1 = start_at <= 1
    need2 = stop_after > 1
    need3 = stop_after >= 3
    wgu_d = [din("wgu1", [44, 128, 16 * 256]) if need1 else None,
             din("wgu2", [44, 128, 16 * 256]) if need3 else None]
    wd_d = [din("wd1", [NQ, NDG, 128, FQ * DGW]) if need1 else None,
            din("wd2", [NQ, NDG, 128, FQ * DGW]) if need3 else None]
    if need2:
        win_d = din("win", [70, 128, 16 * 128])
        woc_d = din("woc", [16, 128, 8 * 128])
        woa_d = din("woa", [16, 128, 8 * 128])
        wo_d = din("wo", [32, 128, 8 * 128])
    y_d = nc.dram_tensor("yT", [128, NCH * TOWN], F32, kind="ExternalOutput").ap().rearrange(
        "p (c t) -> p c t", c=NCH)

    B0 = (nc.sbuf_base + 31) // 32 * 32
    cur = [B0]
    hi_water = [B0]

    def alloc(name, shape, dt, at=None):
        nbytes = int(np.prod(shape[1:])) * (4 if dt == F32 else 2)
        o = cur[0] if at is None else at
        assert o % 32 == 0
        assert o + nbytes <= nc.sbuf_top, f"SBUF overflow at {name}: {o + nbytes - nc.sbuf_top}"
        t = nc.alloc_sbuf_tensor_at(name, shape, dt, offset=o)
        if at is None:
            cur[0] = (o + nbytes + 31) // 32 * 32
            hi_water[0] = max(hi_water[0], cur[0])
        return t

    R = alloc("R", [128, NCH, TT], F32)
    H = alloc("H", [128, NCH, TT], BF16)
    gains = alloc("gains", [128, 48], F32)
    qkg = alloc("qkg", [128, 2], F32)
    convw = alloc("convw", [128, 24], F32)
    sinks = alloc("sinks", [128, 16], F32)
    epsd = alloc("epsd", [128, 1], F32)
    mask = alloc("mask", [128, 2, 256], F32)
    cmb = alloc("cmb", [128, 4, 128], BF16)
    Z0 = cur[0]
    sqF = alloc("sqF", [128, NCH, 512], BF16)
    rstdF = alloc("rstdF", [128, 512], F32)
    sdF = alloc("sdF", [128, 512], F32)
    aT = alloc("aT", [128, FQ, TT], BF16)
    P8 = [alloc(f"P8_{i}", [128, NCH, 256], BF16) for i in range(3)]
    PD = [alloc(f"PD_{i}", [128, FQ, DGW], BF16) for i in range(3)]
    sg = [alloc(f"sg{i}", [128, 512], F32) for i in range(2)]
    cur[0] = Z0
    P4 = [alloc(f"P4_{i}", [128, NCH, 128], BF16) for i in range(3)]
    attn = alloc("attn", [128, 8, TOWN], BF16)
    X0 = cur[0]
    kT2 = alloc("kT2", [128, 4, TT], BF16)
    vpad = alloc("vpad", [128, 9, 4, 2, 128], BF16)
    cb = alloc("cb", [128, 8, TOWN], BF16, at=X0)
    Y0 = cur[0]
    sqM = alloc("sqM", [128, NCH, 512], BF16)
    cs = alloc("cs", [128, 2, TT], F32, at=Y0)
    merged = alloc("merged", [128, 8, TOWN], BF16, at=Y0)
    T0 = cur[0]
    rstdM = alloc("rstdM", [128, 512], F32)
    sdM = alloc("sdM", [128, 512], F32)
    qsq = alloc("qsq", [128, 512], BF16)
    qf = alloc("qf", [128, 512], F32)
    t1r = alloc("t1r", [128, 512], F32)
    t2r = alloc("t2r", [128, 512], F32)
    s_t1 = alloc("s_t", [128, 2, 256], F32)
    p_t1 = alloc("p_t", [128, 2, 256], F32)
    s_t = [s_t1, s_t1]
    p_t = [p_t1, p_t1]
    pn_t = [alloc(f"pn_t{i}", [128, 2, 256], BF16) for i in range(2)]
    pT_t = [alloc(f"pT_t{i}", [128, 4, 128], BF16) for i in range(2)]
    sm_t = [alloc(f"sm_t{i}", [128, 16], F32) for i in range(2)]
    qT = [alloc("qT0", [128, 2, TOWN], BF16, at=Y0 + 9216),
          alloc("qT1", [128, 2, TOWN], BF16)]
    qhi = alloc("qhi", [128, 512], BF16, at=Y0 + 13312)
    qlo = alloc("qlo", [128, 512], BF16, at=Y0 + 14336)
    T_end_m4 = cur[0]
    xc_sb = alloc("xc_sb", [128, TT], F32, at=T0)
    u_sb = alloc("u_sb", [128, TT], F32, at=T0 + 4608)
    y_sb = alloc("y_sb", [128, TOWN], F32, at=T0 + 9216)
    sa = [alloc(f"sa{i}", [128, 512], F32, at=T0 + i * 2048) for i in range(2)]
    sb = [alloc(f"sb{i}", [128, 512], F32, at=T0 + 4096 + i * 2048) for i in range(2)]
    t1m = [alloc(f"t1m{i}", [128, 512], F32, at=T0 + 8192 + i * 2048) for i in range(2)]
    t2m = [alloc(f"t2m{i}", [128, 512], F32, at=T0 + 12288 + i * 2048) for i in range(2)]
    P2 = [alloc(f"P2_{i}", [128, 8, 128], BF16, at=T0 + 16384 + i * 2048) for i in range(4)]
    assert T0 + 16384 + 4 * 2048 <= nc.sbuf_top, "SBUF overflow (M5 temps)"
    assert T_end_m4 <= nc.sbuf_top

    ps = [nc.alloc_psum_tensor(f"ps{b}", [128, 512], F32) for b in range(8)]

    P.add_pool("P8", P8)
    P.add_pool("PD", PD)
    P.add_pool("P4", P4)
    P.add_pool("P2", P2)
    sem_in = {n: P.new_sem(f"in_{n}") for n in
              ("x0", "x1", "x2", "gains", "qkg", "convw", "sinks", "mask", "cmb", "RT", "cs", "out")}

    XK = lambda blk: [("x", c, blk) for c in range(NCH)]
    HK = lambda blk: [("h", c, blk) for c in range(NCH)]

    def load_inputs():
        for blk, (lo, hi) in enumerate(BLK):
            P.dma("sp", R[:, :, lo:hi], xT_d[:, :, lo:hi], sem_in[f"x{blk}"], writes=XK(blk))
        P.dma("sp", gains[:, :], gains_d[:, :], sem_in["gains"], writes=[("gains",)])
        P.dma("sp", mask[:].rearrange("p a b -> p (a b)"), mask_d[:, :], sem_in["mask"], writes=[("mask",)])
        P.dma("sp", qkg[:, :], qkg_d[:, :], sem_in["qkg"], writes=[("qkg",)])
        P.dma("sp", convw[:, :], convw_d[:, :], sem_in["convw"], writes=[("convw",)])
        P.dma("sp", sinks[:, :], sinks_d[:, :], sem_in["sinks"], writes=[("sinks",)])
        P.dma("pool", cmb[:].rearrange("p a b -> p (a b)"), cmat_d[:, 0:512], sem_in["cmb"], writes=[("cmb",)])
        P.op("dve", lambda E: E.memset(epsd[:, :], RMS_EPS), writes=[("eps",)])

    def norm(gi, blocks, sq, rstd, sd):
        for blk in blocks:
            lo, hi = BLK[blk]
            n = hi - lo
            P.act(sq[:, :, 0:n], R[:, :, lo:hi], AF.Square, reads=XK(blk), writes=[("sq",)])
            b = P.nb()
            P.mm([(ps[b][:, 0:n], cmb[:, 0, :], sq[:, c, 0:n], c == 0, c == NCH - 1) for c in range(NCH)],
                 reads=[("sq",), ("cmb",)], writes=[("ps", b)])
            P.act(sd[:, 0:n], ps[b][:, 0:n], AF.Sqrt, reads=[("ps", b), ("eps",)], writes=[("sd",)],
                  bias=epsd[:, 0:1], scale=1.0 / D)
            P.op("dve", lambda E, n=n: E.reciprocal(out=rstd[:, 0:n], in_=sd[:, 0:n]),
                 reads=[("sd",)], writes=[("rstd",)])
            for c in range(NCH):
                P.stt(H[:, c, lo:hi], R[:, c, lo:hi], gains[:, gi * 16 + c:gi * 16 + c + 1], rstd[:, 0:n],
                      ALU.mult, ALU.mult, reads=[("x", c, blk), ("rstd",), ("gains",)], writes=[("h", c, blk)])

    def ffn(idx, blocks):
        norm(0 if idx == 0 else 2, blocks, sqF, rstdF, sdF)
        for qq in range(NQ):
            for f in range(FQ):
                j = qq * FQ + f
                wt, wk = P.use("P8", wgu_d[idx][j, :, :])
                for blk in blocks:
                    lo, hi = BLK[blk]
                    n = hi - lo
                    bg_ = P.nb()
                    P.mm([(ps[bg_][:, 0:n], wt[:, c, 0:128], H[:, c, lo:hi], c == 0, c == NCH - 1)
                          for c in range(NCH)], reads=HK(blk) + [wk], writes=[("ps", bg_)])
                    bu_ = P.nb()
                    P.mm([(ps[bu_][:, 0:n], wt[:, c, 128:256], H[:, c, lo:hi], c == 0, c == NCH - 1)
                          for c in range(NCH)], reads=HK(blk) + [wk], writes=[("ps", bu_)])
                    si = P.flip("sg")
                    P.act(sg[si][:, 0:n], ps[bg_][:, 0:n], AF.Silu, reads=[("ps", bg_)], writes=[("sg", si)])
                    P.tt(aT[:, f, lo:hi], sg[si][:, 0:n], ps[bu_][:, 0:n], ALU.mult,
                         reads=[("sg", si), ("ps", bu_)], writes=[("a", f, blk)])
            for dg in range(NDG):
                wt, wk = P.use("PD", wd_d[idx][qq, dg, :, :])
                for dd in range(DG):
                    dc = dg * DG + dd
                    for blk in blocks:
                        lo, hi = BLK[blk]
                        n = hi - lo
                        b = P.nb()
                        P.mm([(ps[b][:, 0:n], wt[:, f, dd * 128:(dd + 1) * 128], aT[:, f, lo:hi], f == 0, f == FQ - 1)
                              for f in range(FQ)], reads=[("a", f, blk) for f in range(FQ)] + [wk],
                             writes=[("ps", b)])
                        P.stt(R[:, dc, lo:hi], ps[b][:, 0:n], 0.5, R[:, dc, lo:hi], ALU.mult, ALU.add,
                              reads=[("ps", b), ("x", dc, blk)], writes=[("x", dc, blk)])

    def normrope(pb, n, gcol, out_ap, lo, hi, wkey):
        pin = ps[pb][:, 0:n]
        P.act(qsq[:, 0:n], pin, AF.Square, reads=[("ps", pb)], writes=[("qsq",)])
        P.ts(qf[:, 0:n], pin, qkg[:, gcol:gcol + 1], ALU.mult, reads=[("ps", pb), ("qkg",)], writes=[("qf",)])
        b2 = P.nb()
        P.mm([(ps[b2][:, 0:n], cmb[:, 1, :], qsq[:, 0:n], True, True)], reads=[("qsq",), ("cmb",)],
             writes=[("ps", b2)])
        P.op("dve", lambda E: E.tensor_copy(out=qhi[:, 0:n], in_=qf[:, 0:n]), reads=[("qf",)], writes=[("qhi",)])
        P.tt(qlo[:, 0:n], qf[:, 0:n], qhi[:, 0:n], ALU.subtract, reads=[("qf",), ("qhi",)], writes=[("qlo",)])
        b3 = P.nb()
        P.mm([(ps[b3][:, 0:n], cmb[:, 3, :], qhi[:, 0:n], True, False),
              (ps[b3][:, 0:n], cmb[:, 3, :], qlo[:, 0:n], False, True)],
             reads=[("qhi",), ("qlo",), ("cmb",)], writes=[("ps", b3)])
        P.act(sdM[:, 0:n], ps[b2][:, 0:n], AF.Sqrt, reads=[("ps", b2), ("eps",)], writes=[("sd",)],
              bias=epsd[:, 0:1], scale=1.0 / 64)
        P.op("dve", lambda E: E.reciprocal(out=rstdM[:, 0:n], in_=sdM[:, 0:n]), reads=[("sd",)], writes=[("rstd",)])
        P.tt(t1r[:, 0:n], qf[:, 0:n], cs[:, 0, lo:hi], ALU.mult, reads=[("qf",), ("cs",)], writes=[("t1r",)])
        P.tt(t2r[:, 0:n], ps[b3][:, 0:n], cs[:, 1, lo:hi], ALU.mult, reads=[("ps", b3), ("cs",)], writes=[("t2r",)])
        P.tt(t1r[:, 0:n], t1r[:, 0:n], t2r[:, 0:n], ALU.add, reads=[("t1r",), ("t2r",)], writes=[("t1r",)])
        if isinstance(out_ap, list):
            for (dst, p0, p1) in out_ap:
                P.tt(dst, t1r[p0:p1, 0:n], rstdM[p0:p1, 0:n], ALU.mult, reads=[("t1r",), ("rstd",), ("qTz",)],
                     writes=[wkey + (p0,)])
        else:
            P.tt(out_ap, t1r[:, 0:n], rstdM[:, 0:n], ALU.mult, reads=[("t1r",), ("rstd",)], writes=[wkey])

    def kv_phase():
        P.dma("sp", cs[:].rearrange("p a b -> p (a b)"), cs_d[:, :], sem_in["cs"], writes=[("cs",)])
        P.op("dve", lambda E: E.memset(vpad[:].rearrange("p a b c d -> p (a b c d)"), 0.0), writes=[("vpad0",)])
        for hk in range(4):
            wt, wk = P.use("P4", win_d[hk, :, :])
            for blk in range(3):
                lo, hi = BLK[blk]
                n = hi - lo
                b = P.nb()
                P.mm([(ps[b][:, 0:n], wt[:, c, :], H[:, c, lo:hi], c == 0, c == NCH - 1) for c in range(NCH)],
                     reads=HK(blk) + [wk], writes=[("ps", b)])
                normrope(b, n, 1, kT2[:, hk, lo:hi], lo, hi, ("k", hk, blk))
        for vs in range(2):
            wt, wk = P.use("P4", win_d[4 + vs, :, :])
            for t in range(9):
                b = P.nb()
                P.mm([(ps[b][:, 0:128], H[:, c, t * 128:(t + 1) * 128], wt[:, c, :], c == 0, c == NCH - 1)
                      for c in range(NCH)], reads=HK(_tile_blk(t)) + [wk], writes=[("ps", b)])
                src = ps[b][:, 0:128].rearrange("p (e d) -> p e d", e=2)
                P.act(vpad[:, t, 2 * vs:2 * vs + 2, 0, 0:64], src, AF.Copy, reads=[("ps", b), ("vpad0",)],
                      writes=[("v", t, vs, 0)])
                P.op("dve", lambda E, t=t, vs=vs, src=src: E.tensor_copy(out=vpad[:, t, 2 * vs:2 * vs + 2, 1, 64:128],
                                                                         in_=src),
                     reads=[("ps", b), ("vpad0",)], writes=[("v", t, vs, 1)])

    def attn_scores(i, qb, j):
        hk = i // 2
        ab = j % 2
        bs = P.nb()
        kkeys = sorted({("k", hk, _tile_blk(j)), ("k", hk, _tile_blk(j + 1))})
        qblk = 1 if j < 4 else 2
        P.mm([(ps[bs][:, e * 256:(e + 1) * 256], qT[qb][:, e, j * 128:(j + 1) * 128],
               kT2[:, hk, j * 128:j * 128 + 256], True, True) for e in range(2)],
             reads=[("qT", qb, qblk, 0), ("qT", qb, qblk, 64), ("qTz",)] + kkeys, writes=[("ps", bs)])
        sv = s_t[ab]
        sm = sm_t[ab]
        pv = p_t[ab]
        mi = 0 if j == 0 else 1
        for e in range(2):
            P.stt(sv[:, e, :], ps[bs][:, e * 256:(e + 1) * 256], 0.125, mask[:, mi, :], ALU.mult, ALU.add,
                  reads=[("ps", bs), ("mask",)], writes=[("s", e)])
        P.op("dve", lambda E: E.tensor_reduce(out=sm[:, 0:2], in_=sv[:, :, :], axis=AX.X, op=ALU.max),
             reads=[("s", 0), ("s", 1)], writes=[("sm", ab, "mx")])
        P.tt(sm[:, 2:4], sm[:, 0:2], sinks[:, 2 * i:2 * i + 2], ALU.max,
             reads=[("sm", ab, "mx"), ("sinks",)], writes=[("sm", ab, "m")])
        P.tt(sm[:, 6:8], sinks[:, 2 * i:2 * i + 2], sm[:, 2:4], ALU.subtract,
             reads=[("sm", ab, "m"), ("sinks",)], writes=[("sm", ab, "dsk")])
        for e in range(2):
            P.ts(sv[:, e, :], sv[:, e, :], sm[:, 2 + e:3 + e], ALU.subtract,
                 reads=[("s", e), ("sm", ab, "m")], writes=[("s", e)])
        P.act(pv[:].rearrange("p a b -> p (a b)"), sv[:].rearrange("p a b -> p (a b)"), AF.Exp,
              reads=[("s", 0), ("s", 1)], writes=[("p",)])
        P.act(sm[:, 8:10], sm[:, 6:8], AF.Exp, reads=[("sm", ab, "dsk")], writes=[("sm", ab, "es")])
        P.op("dve", lambda E: E.tensor_reduce(out=sm[:, 10:12], in_=pv[:, :, :], axis=AX.X, op=ALU.add),
             reads=[("p",)], writes=[("sm", ab, "rs")])
        P.tt(sm[:, 12:14], sm[:, 10:12], sm[:, 8:10], ALU.add, reads=[("sm", ab, "rs"), ("sm", ab, "es")],
             writes=[("sm", ab, "den")])
        P.op("dve", lambda E: E.reciprocal(out=sm[:, 14:16], in_=sm[:, 12:14]), reads=[("sm", ab, "den")],
             writes=[("sm", ab, "rden")])
        for e in range(2):
            P.ts(pn_t[ab][:, e, :], pv[:, e, :], sm[:, 14 + e:15 + e], ALU.mult,
                 reads=[("p",), ("sm", ab, "rden")], writes=[("pn", ab, e)])

    def attn_pv(i, j):
        hk = i // 2
        ab = j % 2
        bt = P.nb()
        P.mm([(ps[bt][:, (e * 2 + kb) * 128:(e * 2 + kb + 1) * 128], pn_t[ab][:, e, kb * 128:(kb + 1) * 128],
               cmb[:, 2, :], True, True) for e in range(2) for kb in range(2)],
             reads=[("pn", ab, 0), ("pn", ab, 1), ("cmb",)], writes=[("ps", bt)])
        P.act(pT_t[ab][:].rearrange("p a b -> p (a b)"), ps[bt][:, :], AF.Copy, reads=[("ps", bt)],
              writes=[("pT", ab)])
        bo = P.nb()
        mms = []
        for e in range(2):
            for kb in range(2):
                idx = e * 2 + kb
                mms.append((ps[bo][:, 0:128], vpad[:, j + kb, hk, e, :], pT_t[ab][:, idx, :], idx == 0, idx == 3))
        vkeys = [("v", j, hk // 2, 0), ("v", j, hk // 2, 1), ("v", j + 1, hk // 2, 0), ("v", j + 1, hk // 2, 1)]
        P.mm(mms, reads=[("pT", ab)] + vkeys, writes=[("ps", bo)])
        P.act(attn[:, i, j * 128:(j + 1) * 128], ps[bo][:, 0:128], AF.Copy, reads=[("ps", bo)],
              writes=[("attn", i, 1 if j < 4 else 2)])

    def attn_phase():
        for qb_ in range(2):
            P.op("dve", lambda E, qb_=qb_: E.memset(qT[qb_][:].rearrange("p a b -> p (a b)"), 0.0), writes=[("qTz",)])
        for i in range(8):
            qb = i % 2
            wt, wk = P.use("P4", win_d[6 + i, :, :])
            for blk in (1, 2):
                lo, hi = BLK[blk]
                b = P.nb()
                P.mm([(ps[b][:, 0:512], wt[:, c, :], H[:, c, lo:hi], c == 0, c == NCH - 1) for c in range(NCH)],
                     reads=HK(blk) + [wk], writes=[("ps", b)])
                normrope(b, 512, 0, [(qT[qb][0:64, 0, lo - HALO:hi - HALO], 0, 64),
                                     (qT[qb][64:128, 1, lo - HALO:hi - HALO], 64, 128)], lo, hi, ("qT", qb, blk))
            attn_scores(i, qb, 0)
            for j in range(8):
                if j + 1 < 8:
                    attn_scores(i, qb, j + 1)
                attn_pv(i, j)

    def conv_phase():
        for i in range(8):
            wt, wk = P.use("P4", win_d[14 + 3 * i, :, :])
            for blk in range(3):
                lo, hi = BLK[blk]
                n = hi - lo
                b = P.nb()
                P.mm([(ps[b][:, 0:n], wt[:, c, :], H[:, c, lo:hi], c == 0, c == NCH - 1) for c in range(NCH)],
                     reads=HK(blk) + [wk], writes=[("ps", b)])
                P.act(xc_sb[:, lo:hi], ps[b][:, 0:n], AF.Copy, reads=[("ps", b)], writes=[("xc", blk)])
            wt, wk = P.use("P4", win_d[14 + 3 * i + 1, :, :])
            for blk in range(3):
                lo, hi = BLK[blk]
                n = hi - lo
                b = P.nb()
                P.mm([(ps[b][:, 0:n], wt[:, c, :], H[:, c, lo:hi], c == 0, c == NCH - 1) for c in range(NCH)],
                     reads=HK(blk) + [wk], writes=[("ps", b)])
                P.tt(u_sb[:, lo:hi], ps[b][:, 0:n], xc_sb[:, lo:hi], ALU.mult, reads=[("ps", b), ("xc", blk)],
                     writes=[("u", blk)])
            ukeys = [("u", 0), ("u", 1), ("u", 2)]
            P.ts(y_sb[:, :], u_sb[:, 128:1152], convw[:, 3 * i + 2:3 * i + 3], ALU.mult,
                 reads=ukeys + [("convw",)], writes=[("y",)])
            P.stt(y_sb[:, :], u_sb[:, 127:1151], convw[:, 3 * i + 1:3 * i + 2], y_sb[:, :], ALU.mult, ALU.add,
                  reads=ukeys + [("y",), ("convw",)], writes=[("y",)])
            P.stt(y_sb[:, :], u_sb[:, 126:1150], convw[:, 3 * i:3 * i + 1], y_sb[:, :], ALU.mult, ALU.add,
                  reads=ukeys + [("y",), ("convw",)], writes=[("y",)])
            wt, wk = P.use("P4", win_d[14 + 3 * i + 2, :, :])
            for blk in (1, 2):
                lo, hi = BLK[blk]
                b = P.nb()
                P.mm([(ps[b][:, 0:512], wt[:, c, :], H[:, c, lo:hi], c == 0, c == NCH - 1) for c in range(NCH)],
                     reads=HK(blk) + [wk], writes=[("ps", b)])
                P.tt(cb[:, i, lo - HALO:hi - HALO], ps[b][:, 0:512], y_sb[:, lo - HALO:hi - HALO], ALU.mult,
                     reads=[("ps", b), ("y",)], writes=[("cb", i, blk)])

    def merge_phase():
        for mg in range(2):
            P.hold = 3
            for dl in range(8):
                dc = mg * 8 + dl
                w_c, k_c = P.use("P2", woc_d[dc, :, :])
                w_a, k_a = P.use("P2", woa_d[dc, :, :])
                w_ga, k_ga = P.use("P4", win_d[38 + 2 * dc, :, :])
                w_gb, k_gb = P.use("P4", win_d[38 + 2 * dc + 1, :, :])
                for blk in (1, 2):
                    lo, hi = BLK[blk]
                    o = lo - HALO
                    b1 = P.nb()
                    P.mm([(ps[b1][:, :], w_c[:, c, :], cb[:, c, o:o + 512], c == 0, c == 7) for c in range(8)],
                         reads=[("cb", c, blk) for c in range(8)] + [k_c], writes=[("ps", b1)])
                    b2 = P.nb()
                    P.mm([(ps[b2][:, :], w_a[:, c, :], attn[:, c, o:o + 512], c == 0, c == 7) for c in range(8)],
                         reads=[("attn", c, blk) for c in range(8)] + [k_a], writes=[("ps", b2)])
                    b3 = P.nb()
                    P.mm([(ps[b3][:, :], w_ga[:, c, :], H[:, c, lo:hi], c == 0, c == NCH - 1) for c in range(NCH)],
                         reads=HK(blk) + [k_ga], writes=[("ps", b3)])
                    b4 = P.nb()
                    P.mm([(ps[b4][:, :], w_gb[:, c, :], H[:, c, lo:hi], c == 0, c == NCH - 1) for c in range(NCH)],
                         reads=HK(blk) + [k_gb], writes=[("ps", b4)])
                    mb = P.flip("mb")
                    P.act(sa[mb][:, :], ps[b3][:, :], AF.Sigmoid, reads=[("ps", b3)], writes=[("sa", mb)])
                    P.act(sb[mb][:, :], ps[b4][:, :], AF.Sigmoid, reads=[("ps", b4)], writes=[("sb", mb)])
                    P.tt(t1m[mb][:, :], sa[mb][:, :], ps[b1][:, :], ALU.mult, reads=[("sa", mb), ("ps", b1)],
                         writes=[("t1m", mb)])
                    P.tt(t2m[mb][:, :], sb[mb][:, :], ps[b2][:, :], ALU.mult, reads=[("sb", mb), ("ps", b2)],
                         writes=[("t2m", mb)])
                    P.tt(merged[:, dl, o:o + 512], t1m[mb][:, :], t2m[mb][:, :], ALU.add,
                         reads=[("t1m", mb), ("t2m", mb)], writes=[("mg", dl, blk)])
            P.hold = 0
            for dc2 in range(NCH):
                w_o, k_o = P.use("P2", wo_d[mg * 16 + dc2, :, :])
                for blk in (1, 2):
                    lo, hi = BLK[blk]
                    o = lo - HALO
                    b = P.nb()
                    P.mm([(ps[b][:, :], w_o[:, dl, :], merged[:, dl, o:o + 512], dl == 0, dl == 7) for dl in range(8)],
                         reads=[("mg", dl, blk) for dl in range(8)] + [k_o], writes=[("ps", b)])
                    P.tt(R[:, dc2, lo:hi], ps[b][:, :], R[:, dc2, lo:hi], ALU.add,
                         reads=[("ps", b), ("x", dc2, blk)], writes=[("x", dc2, blk)])

    def store_output():
        ev = P.dma("sp", y_d[:, :, :], R[:, :, HALO:TT], sem_in["out"], reads=XK(1) + XK(2))
        P.wait_only("sp", ev if ev is not None else (sem_in["out"], 16))

    def program():
        load_inputs()
        if start_at <= 1:
            ffn(0, [0, 1, 2])
        if stop_after >= 1.2:
            P.fence()
            norm(1, [0, 1, 2], sqM, rstdM, sdM)
        if stop_after >= 1.4:
            P.fence()
            kv_phase()
        if stop_after >= 1.6:
            attn_phase()
        if stop_after >= 1.8:
            P.fence()
            conv_phase()
        if stop_after >= 2:
            P.fence()
            merge_phase()
        if stop_after >= 3:
            P.fence()
            ffn(1, [1, 2])
        store_output()

    program()
    P.start_emit()
    program()

    with nc.Block() as block:
        @block.tensor
        def _(E):
            P.emit_engine("pe", E)

        @block.scalar
        def _(E):
            P.emit_engine("act", E)

        @block.vector
        def _(E):
            P.emit_engine("dve", E)

        @block.gpsimd
        def _(E):
            P.emit_engine("pool", E)

        @block.sync
        def _(E):
            P.emit_engine("sp", E)
    es.close()
    return nc


def _fm(a2d):
    t = a2d.shape[0]
    return np.ascontiguousarray(a2d.T.reshape(NCH, 128, t).transpose(1, 0, 2)).reshape(128, NCH * t)


def _colslab(w, cols):
    nci = w.shape[0] // 128
    return np.ascontiguousarray(w[:, cols].reshape(nci, 128, len(cols)).transpose(1, 0, 2)).reshape(128, -1)


def _prep_weights(w_gu, w_down):
    g4 = w_gu[:, :DFF].reshape(NCH, 128, 44, 128)
    u4 = w_gu[:, DFF:].reshape(NCH, 128, 44, 128)
    gu = np.stack([g4, u4], axis=3)
    wgu_t = np.ascontiguousarray(gu.transpose(2, 1, 0, 3, 4)).reshape(44, 128, NCH * 256)
    w4 = w_down.reshape(NQ, FQ, 128, NDG, DGW)
    wd_t = np.ascontiguousarray(w4.transpose(0, 3, 2, 1, 4)).reshape(NQ, NDG, 128, FQ * DGW)
    return wgu_t, wd_t


def prepare_in_maps(x, g_ffn1, w_gu1, w_down1, g_mix, w_in, conv_w, q_norm_g, k_norm_g, sinks,
                    w_out_conv, w_out_attn, w_o, g_ffn2, w_gu2, w_down2):
    f32 = np.float32
    x = np.asarray(x, f32)
    w_in0 = np.asarray(w_in, f32)[0]
    ar = np.arange
    shared = {}
    if START_AT <= 1:
        shared["wgu1"], shared["wd1"] = _prep_weights(np.asarray(w_gu1, f32)[0], np.asarray(w_down1, f32)[0])
    if STOP_AFTER >= 3:
        shared["wgu2"], shared["wd2"] = _prep_weights(np.asarray(w_gu2, f32)[0], np.asarray(w_down2, f32)[0])
    slabs = []
    for hk in range(4):
        cols = np.concatenate([4096 + hk * 64 + ar(64)] * 2)
        slabs.append(_colslab(w_in0, cols))
    for vs in range(2):
        slabs.append(_colslab(w_in0, 4352 + vs * 128 + ar(128)))
    for i in range(8):
        slabs.append(_colslab(w_in0, 3072 + i * 128 + ar(128)))
    for i in range(8):
        slabs.append(_colslab(w_in0, 0 + i * 128 + ar(128)))
        slabs.append(_colslab(w_in0, 2048 + i * 128 + ar(128)))
        slabs.append(_colslab(w_in0, 1024 + i * 128 + ar(128)))
    for dc in range(NCH):
        slabs.append(_colslab(w_in0, 4608 + dc * 128 + ar(128)))
        slabs.append(_colslab(w_in0, 6656 + dc * 128 + ar(128)))
    woc = np.asarray(w_out_conv, f32)[0]
    woa = np.asarray(w_out_attn, f32)[0]
    wo = np.asarray(w_o, f32)[0]
    if STOP_AFTER > 1:
        shared["win"] = np.stack(slabs)
        shared["woc"] = np.stack([_colslab(woc, dc * 128 + ar(128)) for dc in range(NCH)])
        shared["woa"] = np.stack([_colslab(woa, dc * 128 + ar(128)) for dc in range(NCH)])
        shared["wo"] = np.stack([_colslab(wo[mg * 1024:(mg + 1) * 1024], dc2 * 128 + ar(128))
                                 for mg in range(2) for dc2 in range(NCH)])
    gains = np.concatenate([np.asarray(g, f32)[0].reshape(NCH, 128).T for g in (g_ffn1, g_mix, g_ffn2)], axis=1)
    shared["gains"] = np.ascontiguousarray(gains)
    qg = np.asarray(q_norm_g, f32)[0]
    kg = np.asarray(k_norm_g, f32)[0]
    shared["qkg"] = np.ascontiguousarray(np.stack([np.tile(qg, 2), np.tile(kg, 2)], axis=1))
    shared["convw"] = np.ascontiguousarray(
        np.asarray(conv_w, f32)[0].reshape(3, 8, 128).transpose(2, 1, 0)).reshape(128, 24)
    shared["sinks"] = np.ascontiguousarray(np.broadcast_to(np.asarray(sinks, f32)[0][None, :], (128, 16)))
    ones = np.ones((128, 128), f32)
    blockones = np.zeros((128, 128), f32)
    blockones[:64, :64] = 1
    blockones[64:, 64:] = 1
    ident = np.eye(128, dtype=f32)
    RT = np.zeros((128, 128), f32)
    for hb in (0, 64):
        for i in range(8):
            RT[hb + i + 8, hb + i] = -1.0
            RT[hb + i, hb + i + 8] = 1.0
    shared["cmat"] = np.ascontiguousarray(np.concatenate([ones, blockones, ident, RT], axis=1))
    inv_freq = (1.0 / (np.float32(500000.0) ** (np.arange(0, 16, 2, dtype=f32) / np.float32(16)))).astype(f32)
    qi = np.arange(128)[:, None]
    kk = np.arange(128)[None, :]
    m_prev = np.where(kk > qi, 0.0, NEG).astype(f32)
    m_cur = np.where(kk <= qi, 0.0, NEG).astype(f32)
    m_norm = np.concatenate([m_prev, m_cur], axis=1)
    m_first = np.concatenate([np.full((128, 128), NEG, f32), m_cur], axis=1)

    in_maps = []
    for c in range(NCORES):
        b, hf = c // 2, c % 2
        own = x[b, hf * TOWN:(hf + 1) * TOWN]
        halo = x[b, TOWN - HALO:TOWN] if hf == 1 else np.zeros((HALO, D), f32)
        m = dict(shared)
        m["xT"] = _fm(np.concatenate([halo, own], axis=0))
        pos = np.maximum(hf * TOWN - HALO + np.arange(TT), 0).astype(f32)
        ang = pos[:, None] * inv_freq[None, :]
        cos8, sin8 = np.cos(ang).astype(f32), np.sin(ang).astype(f32)
        ct = np.ones((128, TT), f32)
        st = np.zeros((128, TT), f32)
        for hb in (0, 64):
            for d in range(16):
                ct[hb + d] = cos8[:, d % 8]
                st[hb + d] = sin8[:, d % 8]
        m["cs"] = np.ascontiguousarray(np.concatenate([ct, st], axis=1))
        m["mask"] = np.ascontiguousarray(np.concatenate([m_norm if hf == 1 else m_first, m_norm], axis=1))
        in_maps.append(m)
    return in_maps


def kernel(**inputs):
    f32 = np.float32
    in_maps = prepare_in_maps(**inputs)
    nc = build_nc(STOP_AFTER, START_AT)
    res = run_bass_kernel_spmd(nc, in_maps, core_ids=list(range(NCORES)))
    out = np.zeros((BATCH, SEQ, D), f32) if NCORES < 8 else np.empty((BATCH, SEQ, D), f32)
    for c in range(NCORES):
        b, hf = c // 2, c % 2
        yT = np.asarray(res.results[c]["yT"]).reshape(128, NCH, TOWN)
        out[b, hf * TOWN:(hf + 1) * TOWN] = yT.transpose(2, 1, 0).reshape(TOWN, D)
    return out
```
